# Optimizing a Trainium2 kernel written in Bass

```python
import math
import jax, jax.numpy as jnp
from jax import lax
import numpy as np


D_MODEL = 4096
BATCH = 1
SEQ = 8192
DEPTH = 4

CHUNK = 64
Q_BLOCK = 128
V_HEAD_DIM = 128
N_HEADS = D_MODEL // V_HEAD_DIM
QK_NOPE_DIM = 128
QK_ROPE_DIM = 64
QK_HEAD_DIM = QK_NOPE_DIM + QK_ROPE_DIM
Q_LORA_RANK = 1024
KV_LORA_RANK = 512
ROPE_THETA = 10000.0
CONV_DIM = D_MODEL
CONV_WIDTH = 3
D_FF = 2 * D_MODEL
DEEPNORM_ALPHA = (2.0 * DEPTH) ** 0.25
DEEPNORM_BETA = (8.0 * DEPTH) ** -0.25
LN_EPS = 1e-5
RMS_EPS = 1e-6

OFF_Q = 0
OFF_KV = OFF_Q + Q_LORA_RANK
OFF_KR = OFF_KV + KV_LORA_RANK
OFF_CB = OFF_KR + QK_ROPE_DIM
OFF_CC = OFF_CB + CONV_DIM
OFF_CH = OFF_CC + CONV_DIM
OFF_GA = OFF_CH + CONV_DIM
OFF_GC = OFF_GA + D_MODEL
N_IN = OFF_GC + D_MODEL

kernel_name = "hybrid_mla_shortconv_convffn_deepnorm"


def rmsnorm(x, w):
    xf = x.astype(jnp.float32)
    y = xf * lax.rsqrt(jnp.mean(xf * xf, axis=-1, keepdims=True) + RMS_EPS)
    return (y * w.astype(jnp.float32)).astype(x.dtype)


def layernorm(x, g, b):
    xf = x.astype(jnp.float32)
    mu = jnp.mean(xf, axis=-1, keepdims=True)
    var = jnp.mean(jnp.square(xf - mu), axis=-1, keepdims=True)
    y = (xf - mu) * lax.rsqrt(var + LN_EPS)
    return (y * g.astype(jnp.float32) + b.astype(jnp.float32)).astype(x.dtype)


def rope_tables(positions):
    inv_freq = 1.0 / (ROPE_THETA ** (jnp.arange(0, QK_ROPE_DIM, 2, dtype=jnp.float32) / QK_ROPE_DIM))
    ang = positions.astype(jnp.float32)[..., None] * inv_freq
    return jnp.cos(ang), jnp.sin(ang)


def apply_rope(x, cos, sin):
    half = x.shape[-1] // 2
    x1, x2 = x[..., :half], x[..., half:]
    c, s = cos.astype(x.dtype), sin.astype(x.dtype)
    return jnp.concatenate([x1 * c - x2 * s, x2 * c + x1 * s], axis=-1)


def causal_dwconv3(u, w):
    S = u.shape[1]
    up = jnp.pad(u, ((0, 0), (CONV_WIDTH - 1, 0), (0, 0)))
    return up[:, 0:S] * w[0] + up[:, 1:S + 1] * w[1] + up[:, 2:S + 2] * w[2]


def mla_branch(c_q, c_kv, k_rope_raw, cos, sin, q_norm_w, kv_norm_w, w_uq, w_ukv):
    B, S, _ = c_q.shape
    q = (rmsnorm(c_q, q_norm_w) @ w_uq).reshape(B, S, N_HEADS, QK_HEAD_DIM)
    q_nope, q_rope = q[..., :QK_NOPE_DIM], q[..., QK_NOPE_DIM:]
    q_rope = apply_rope(q_rope, cos[:, :, None, :], sin[:, :, None, :])
    kv = (rmsnorm(c_kv, kv_norm_w) @ w_ukv).reshape(B, S, N_HEADS, QK_NOPE_DIM + V_HEAD_DIM)
    k_nope, v = kv[..., :QK_NOPE_DIM], kv[..., QK_NOPE_DIM:]
    k_rope = apply_rope(k_rope_raw, cos, sin)

    nb = S // Q_BLOCK
    qn_b = q_nope.reshape(B, nb, Q_BLOCK, N_HEADS, QK_NOPE_DIM).transpose(1, 0, 2, 3, 4)
    qr_b = q_rope.reshape(B, nb, Q_BLOCK, N_HEADS, QK_ROPE_DIM).transpose(1, 0, 2, 3, 4)
    k_chunk = jnp.arange(S) // CHUNK
    scale = 1.0 / math.sqrt(QK_HEAD_DIM)

    def block(args):
        qn, qr, bi = args
        s = (jnp.einsum('bqhd,bkhd->bhqk', qn, k_nope)
             + jnp.einsum('bqhr,bkr->bhqk', qr, k_rope))
        q_chunk = (bi * Q_BLOCK + jnp.arange(Q_BLOCK)) // CHUNK
        mask = k_chunk[None, :] <= q_chunk[:, None]
        s = jnp.where(mask, s.astype(jnp.float32) * scale, -jnp.inf)
        p = jax.nn.softmax(s, axis=-1).astype(v.dtype)
        return jnp.einsum('bhqk,bkhd->bqhd', p, v)

    o = lax.map(block, (qn_b, qr_b, jnp.arange(nb)))
    return o.transpose(1, 0, 2, 3, 4).reshape(B, S, N_HEADS * V_HEAD_DIM)


def setup_inputs(seed: int = 0) -> dict:
    key = jax.random.key(seed)
    ks = jax.random.split(key, 18)
    f32 = jnp.float32
    nrm = lambda k, shape, s: jax.random.normal(k, shape, f32) * s
    x = jax.random.normal(ks[0], (BATCH, SEQ, D_MODEL), f32)
    positions = jnp.broadcast_to(jnp.arange(SEQ, dtype=jnp.int32)[None, :], (BATCH, SEQ))
    return {
        "x": x,
        "positions": positions,
        "w_in": nrm(ks[1], (DEPTH, D_MODEL, N_IN), D_MODEL ** -0.5),
        "b_gate": nrm(ks[2], (DEPTH, 2 * D_MODEL), 0.02),
        "q_norm_w": 1.0 + nrm(ks[3], (DEPTH, Q_LORA_RANK), 0.02),
        "kv_norm_w": 1.0 + nrm(ks[4], (DEPTH, KV_LORA_RANK), 0.02),
        "w_uq": nrm(ks[5], (DEPTH, Q_LORA_RANK, N_HEADS * QK_HEAD_DIM), Q_LORA_RANK ** -0.5),
        "w_ukv": nrm(ks[6], (DEPTH, KV_LORA_RANK, N_HEADS * (QK_NOPE_DIM + V_HEAD_DIM)), KV_LORA_RANK ** -0.5),
        "conv_w": nrm(ks[7], (DEPTH, CONV_WIDTH, CONV_DIM), CONV_WIDTH ** -0.5),
        "w_o": nrm(ks[8], (DEPTH, D_MODEL, D_MODEL), D_MODEL ** -0.5 * DEEPNORM_BETA),
        "ln1_g": 1.0 + nrm(ks[9], (DEPTH, D_MODEL), 0.02),
        "ln1_b": nrm(ks[10], (DEPTH, D_MODEL), 0.02),
        "w_ffn_in": nrm(ks[11], (DEPTH, D_MODEL, 2 * D_FF), D_MODEL ** -0.5),
        "ffn_conv_w": nrm(ks[12], (DEPTH, CONV_WIDTH, D_FF), CONV_WIDTH ** -0.5),
        "w_ffn_down": nrm(ks[13], (DEPTH, D_FF, D_MODEL), D_FF ** -0.5 * DEEPNORM_BETA),
        "ln2_g": 1.0 + nrm(ks[14], (DEPTH, D_MODEL), 0.02),
        "ln2_b": nrm(ks[15], (DEPTH, D_MODEL), 0.02),
    }


def reference(x, positions, w_in, b_gate, q_norm_w, kv_norm_w, w_uq, w_ukv, conv_w, w_o,
              ln1_g, ln1_b, w_ffn_in, ffn_conv_w, w_ffn_down, ln2_g, ln2_b):
    cos, sin = rope_tables(positions)
    for l in range(DEPTH):
        proj = x @ w_in[l]
        c_q = proj[..., OFF_Q:OFF_KV]
        c_kv = proj[..., OFF_KV:OFF_KR]
        k_rope_raw = proj[..., OFF_KR:OFF_CB]
        gate_b = proj[..., OFF_CB:OFF_CC]
        gate_c = proj[..., OFF_CC:OFF_CH]
        h = proj[..., OFF_CH:OFF_GA]
        gates = jax.nn.sigmoid(proj[..., OFF_GA:N_IN] + b_gate[l])
        g_attn, g_conv = gates[..., :D_MODEL], gates[..., D_MODEL:]

        attn = mla_branch(c_q, c_kv, k_rope_raw, cos, sin,
                          q_norm_w[l], kv_norm_w[l], w_uq[l], w_ukv[l])
        conv = gate_b * causal_dwconv3(gate_c * h, conv_w[l])

        mixed = (g_attn * attn + g_conv * conv) @ w_o[l]
        x = layernorm(DEEPNORM_ALPHA * x + mixed, ln1_g[l], ln1_b[l])

        up = x @ w_ffn_in[l]
        a, u = up[..., :D_FF], up[..., D_FF:]
        f = (jax.nn.silu(causal_dwconv3(a, ffn_conv_w[l])) * u) @ w_ffn_down[l]
        x = layernorm(DEEPNORM_ALPHA * x + f, ln2_g[l], ln2_b[l])
    return x
```

```python
import math
import numpy as np
import concourse.bass as bass
import concourse.mybir as mybir
from concourse.bass_utils import run_bass_kernel_spmd

F32 = mybir.dt.float32
BF16 = mybir.dt.bfloat16
I32 = mybir.dt.int32
ALU = mybir.AluOpType
AF = mybir.ActivationFunctionType

S = 8192
D = 4096
L = 4
NCORE = 8
TPC = S // NCORE
TB = 512
NTB = S // TB
NQ, NKV, NR = 1024, 512, 64
DFF = 8192
OFF_Q = 0
OFF_KV = 1024
OFF_KR = 1536
OFF_CB = 1600
OFF_CC = OFF_CB + D
OFF_CH = OFF_CC + D
OFF_GA = OFF_CH + D
OFF_GC = OFF_GA + D
ALPHA = (2.0 * L) ** 0.25
LN_EPS = 1e-5
RMS_EPS = 1e-6
SCALE = 1.0 / math.sqrt(192.0)
THETA = 10000.0
NB1 = 33
PI = math.pi


class Sem:
    __slots__ = ("h", "v", "name")

    def __init__(self, h, name):
        self.h = h
        self.v = 0
        self.name = name


class Buf:
    __slots__ = ("ap", "name", "w", "r", "pw", "pr")

    def __init__(self, ap, name):
        self.ap = ap
        self.name = name
        self.w = {}
        self.r = {}
        self.pw = {}
        self.pr = {}


def _merge(dst, src):
    for k, v in src.items():
        if dst.get(k, 0) < v:
            dst[k] = v


CHECK = False


class Prog:
    ENG = ("sync", "gpsimd", "tensor", "vector", "scalar")

    def __init__(self, nc):
        self.nc = nc
        self.sem_handles = {}
        self.cur = None
        self.e = None

    def begin_pass(self, name, eng):
        self.cur = name
        self.e = eng
        self.sems = {}
        self.seen = {n: {} for n in self.ENG}
        self.prog_sem = {n: self.sem("prog_" + n) for n in self.ENG}
        self.bufs = []
        self.chk = CHECK and name == "sync"
        self.vc = {n: {} for n in self.ENG}
        self.snap = {}
        self.logs = {}
        self.nviol = 0

    def sem(self, name):
        s = self.sems.get(name)
        if s is None:
            h = self.sem_handles.get(name)
            if h is None:
                h = self.nc.alloc_semaphore(name)
                self.sem_handles[name] = h
            s = Sem(h, name)
            self.sems[name] = s
        return s

    def buf(self, ap, name):
        b = Buf(ap, name)
        self.bufs.append(b)
        return b

    def wait_toks(self, eng, toks):
        seen = self.seen[eng]
        own = self.prog_sem[eng]
        for s, v in toks.items():
            if s is own:
                continue
            if seen.get(s, 0) >= v:
                continue
            seen[s] = v
            if self.chk:
                _merge(self.vc[eng], self.snap[(s, v)])
            if self.cur == eng:
                self.e.wait_ge(s.h, v)

    def deps(self, eng, reads=(), writes=(), partial=False):
        for b in reads:
            self.wait_toks(eng, b.w)
        for b in writes:
            if b.r:
                b.pr = b.r
                b.pw = b.w
                b.r = {}
                b.w = {}
            self.wait_toks(eng, b.pr)
            self.wait_toks(eng, b.pw)
            if not partial:
                self.wait_toks(eng, b.w)

    def record(self, tok, reads=(), writes=(), eng=None, partial=False):
        for b in reads:
            _merge(b.r, tok)
        for b in writes:
            _merge(b.w, tok)
        if self.chk:
            (ts, tv), = tok.items()
            vcn = self.vc[eng]
            isdma = ts.name.startswith("d_") or ts.name.startswith("cc")
            for kind, lst in (("r", reads), ("w", writes)):
                for b in lst:
                    lg = self.logs.setdefault(id(b), [])
                    psum = b.name.startswith("ps")
                    for (k2, e2, s2, v2, p2) in lg:
                        if k2 == "r" and kind == "r" and not psum:
                            continue
                        if k2 == "w" and kind == "w" and partial:
                            continue
                        if e2 == eng and not (isdma or s2.name.startswith("d_") or s2.name.startswith("cc")):
                            continue
                        if vcn.get(s2, 0) >= v2:
                            continue
                        self.nviol += 1
                        key = (b.name, kind, eng, k2, e2)
                        self.vkinds = getattr(self, "vkinds", {})
                        self.vkinds[key] = self.vkinds.get(key, 0) + 1
                        if self.vkinds[key] <= 2:
                            print(f"VIOLATION buf={b.name} {kind} by {eng} tok=({ts.name},{tv}) vs earlier {k2} by {e2} tok=({s2.name},{v2}) known={vcn.get(s2, 0)}")
                    lg.append((kind, eng, ts, tv, partial))
                    if len(lg) > 24:
                        del lg[0]

    def emit(self, eng, fn):
        if self.cur == eng:
            return fn(self.e)
        return None

    def tick(self, eng, inst):
        s = self.prog_sem[eng]
        s.v += 1
        if self.cur == eng:
            inst.then_inc(s.h, 1)
        if self.chk:
            self.vc[eng][s] = s.v
            self.snap[(s, s.v)] = dict(self.vc[eng])
        return {s: s.v}

    def op(self, eng, fn, reads=(), writes=(), partial=False, fence=False):
        self.deps(eng, reads, writes, partial)
        if fence:
            own = self.prog_sem[eng]
            if own.v > 0 and self.cur == eng:
                self.e.wait_ge(own.h, own.v)
        inst = self.emit(eng, fn)
        tok = self.tick(eng, inst)
        self.record(tok, reads, writes, eng=eng, partial=partial)
        return tok

    def dma(self, q, out, in_, reads=(), writes=(), semname=None, partial=False):
        self.deps(q, reads, writes, partial)
        if semname is None:
            semname = "d_" + (writes[0].name if writes and writes[0].name[0] != "@" else reads[0].name)
        s = self.sem(semname)
        s.v += 16
        if self.cur == q:
            self.e.dma_start(out=out, in_=in_).then_inc(s.h, 16)
        tok = {s: s.v}
        if self.chk:
            sn = dict(self.vc[q])
            sn[s] = s.v
            self.snap[(s, s.v)] = sn
        self.record(tok, reads, writes, eng=q, partial=partial)
        return tok

    def cc(self, kind, op, in_ap, out_ap, reads=(), writes=(), semname="cc"):
        q = "gpsimd"
        self.deps(q, reads, writes)
        s = self.sem(semname)
        s.v += 1
        if self.cur == q:
            self.e.collective_compute(kind, op, replica_groups=[list(range(NCORE))],
                                      ins=[in_ap], outs=[out_ap]).then_inc(s.h, 1)
        tok = {s: s.v}
        if self.chk:
            sn = dict(self.vc[q])
            sn[s] = s.v
            self.snap[(s, s.v)] = sn
        self.record(tok, reads, writes, eng=q)
        return tok

    def barrier(self):
        allt = {}
        for s in self.sems.values():
            if s.v > 0:
                allt[s] = s.v
        for n in self.ENG:
            self.wait_toks(n, allt)
        for b in self.bufs:
            b.w = {}
            b.r = {}
            b.pw = {}
            b.pr = {}
        self.logs = {}

    def final_wait(self):
        allt = {s: s.v for s in self.sems.values() if s.v > 0}
        for n in self.ENG:
            self.wait_toks(n, allt)


class Arena:
    def __init__(self, P, ap, nbytes):
        self.P = P
        self.base = ap
        self.nbytes = nbytes
        self.off = 0
        self.peak = 0

    def mark(self):
        return self.off

    def release(self, m):
        self.off = m

    def alloc(self, name, shape, dtype, parts=128):
        esz = 4 if dtype in (F32, I32) else 2
        n = 1
        for d in shape:
            n *= d
        nb = (n * esz + 31) // 32 * 32
        assert self.off + nb <= self.nbytes, f"SBUF arena overflow at {name}: {self.off}+{nb} > {self.nbytes}"
        v = self.base[0:parts, self.off // 2:(self.off + n * esz) // 2]
        if dtype != BF16:
            v = v.bitcast(dtype)
        if len(shape) == 2:
            v = v.rearrange("p (a b) -> p a b", a=shape[0])
        elif len(shape) == 3:
            v = v.rearrange("p (a b c) -> p a b c", a=shape[0], b=shape[1])
        self.off += nb
        self.peak = max(self.peak, self.off)
        return self.P.buf(v, name)


def build(nlayers=L, debug=False):
    nc = bass.Bass("TRN2", target_bir_lowering=False)

    def din(name, shape, dt=F32):
        return nc.dram_tensor(name, list(shape), dt, kind="ExternalInput").ap()

    NL = nlayers
    xs = din("xs", [TPC, D])
    pos = din("pos", [64, S], I32)
    w1 = din("w1", [NL * NB1, 128, 32 * 128])
    wuq = din("wuq", [NL * 4, 128, 8 * 256])
    wukv = din("wukv", [NL * 4, 128, 4 * 256])
    wo = din("wo", [NL, 128, 4 * D])
    w5 = din("w5", [NL * 16, 128, 32 * 128])
    wd = din("wd", [NL, 128, 8 * D])
    bg = din("bg", [128, NL * 8])
    nw = din("nw", [128, NL * 12])
    cw = din("cw", [128, NL * 12])
    fcw = din("fcw", [128, NL * 24])
    lnp = din("lnp", [NL * 4, 128, D])
    cst_ident = din("ident", [128, 128])
    cst_mask = din("mask", [128, 4 * TB])
    cst_invf = din("invf", [64, 1])
    y = nc.dram_tensor("y", [TPC, D], F32, kind="ExternalOutput").ap()

    def dint(name, shape, dt):
        return nc.dram_tensor(name, list(shape), dt).ap()

    w1q = dint("w1q", [NL * NB1, 128, 32 * 128], BF16)
    w5q = dint("w5q", [NL * 16, 128, 32 * 128], BF16)
    ag_in2 = [dint(f"ag_in{i}", [2 * 128, 32 * TB], BF16) for i in range(2)]
    xg2 = [dint(f"xg{i}", [NCORE * 2 * 128, 32 * TB], BF16) for i in range(2)]
    ag_in = [a.rearrange("(n p) f -> n p f", p=128) for a in ag_in2]
    xg = [a.rearrange("(n p) f -> n p f", p=128) for a in xg2]
    cqn = dint("cqn", [NTB, 128, 8 * TB], BF16)
    ckvn = dint("ckvn", [NTB, 128, 4 * TB], BF16)
    mconv = dint("mconv", [NTB, 128, 4 * TB], BF16)
    gattn = dint("gattn", [NTB, 128, 4 * TB], BF16)
    rs_in = dint("rs_in", [S, D], F32)
    rs_out = dint("rs_out", [TPC, D], F32)
    xres = dint("xres", [TPC, D], F32)
    cstab = dint("cstab", [2, 64, S], F32)

    dbg = {}
    if debug:
        dbg["cqn"] = nc.dram_tensor("dbg_cqn", [NTB, 128, 8 * TB], BF16, kind="ExternalOutput").ap()
        dbg["ckvn"] = nc.dram_tensor("dbg_ckvn", [NTB, 128, 4 * TB], BF16, kind="ExternalOutput").ap()
        dbg["mconv"] = nc.dram_tensor("dbg_mconv", [NTB, 128, 4 * TB], BF16, kind="ExternalOutput").ap()
        dbg["gattn"] = nc.dram_tensor("dbg_gattn", [NTB, 128, 4 * TB], BF16, kind="ExternalOutput").ap()
        dbg["kr"] = nc.dram_tensor("dbg_kr", [64, S], BF16, kind="ExternalOutput").ap()
        dbg["minT"] = nc.dram_tensor("dbg_minT", [128, 4 * S], BF16, kind="ExternalOutput").ap()
        dbg["x1"] = nc.dram_tensor("dbg_x1", [TPC, D], F32, kind="ExternalOutput").ap()
        dbg["rs1"] = nc.dram_tensor("dbg_rs1", [TPC, D], F32, kind="ExternalOutput").ap()
        dbg["cstab"] = nc.dram_tensor("dbg_cstab", [2, 64, S], F32, kind="ExternalOutput").ap()

    ARENA_BYTES = 190 * 1024
    P = Prog(nc)

    def program(P, arena_ap, psum_aps):
        A = Arena(P, arena_ap, ARENA_BYTES)
        ps = [P.buf(psum_aps[i], f"ps{i}") for i in range(8)]

        def D_(ap, name):
            return P.buf(ap, "@" + name)

        B_w1q = [D_(w1q, f"w1q{l}") for l in range(NL)]
        B_w5q = [D_(w5q, f"w5q{l}") for l in range(NL)]
        B_agin = [D_(ag_in[i], f"agin{i}") for i in range(2)]
        B_xg = [D_(xg[i], f"xg{i}") for i in range(2)]
        B_cqn = D_(cqn, "cqn")
        B_ckvn = D_(ckvn, "ckvn")
        B_mconv = D_(mconv, "mconv")
        B_gattn = D_(gattn, "gattn")
        B_rsin = D_(rs_in, "rsin")
        B_rsout = D_(rs_out, "rsout")
        B_xres = D_(xres, "xres")
        B_cstab = D_(cstab, "cstab")
        B_y = D_(y, "y")
        B_dbg = {k: D_(v, "dbg" + k) for k, v in dbg.items()}

        ident = A.alloc("ident", [128], BF16)
        ones = A.alloc("ones", [128], BF16)
        mask = A.alloc("mask", [4, TB], BF16)
        prm_bg = A.alloc("prm_bg", [NL * 8], F32)
        prm_nw = A.alloc("prm_nw", [NL * 12], F32)
        prm_cw = A.alloc("prm_cw", [NL * 12], F32)
        prm_fcw = A.alloc("prm_fcw", [NL * 24], F32)
        c_rms = A.alloc("c_rms", [1], F32)
        c_ln = A.alloc("c_ln", [1], F32)
        persist_mark = A.mark()

        m0 = A.mark()
        tmpf = A.alloc("tmpf", [4 * TB], F32)
        P.dma("sync", tmpf.ap[:, 0:128], cst_ident, writes=[tmpf])
        P.op("vector", lambda e: e.tensor_copy(out=ident.ap, in_=tmpf.ap[:, 0:128]), reads=[tmpf], writes=[ident])
        P.dma("sync", tmpf.ap, cst_mask, writes=[tmpf])
        P.op("vector", lambda e: e.tensor_copy(out=mask.ap.rearrange("p a b -> p (a b)"), in_=tmpf.ap), reads=[tmpf], writes=[mask])
        P.op("vector", lambda e: e.memset(ones.ap, 1.0), writes=[ones])
        P.op("vector", lambda e: e.memset(c_rms.ap, RMS_EPS), writes=[c_rms])
        P.op("vector", lambda e: e.memset(c_ln.ap, LN_EPS), writes=[c_ln])
        P.dma("sync", prm_bg.ap, bg, writes=[prm_bg])
        P.dma("sync", prm_nw.ap, nw, writes=[prm_nw])
        P.dma("sync", prm_cw.ap, cw, writes=[prm_cw])
        P.dma("sync", prm_fcw.ap, fcw, writes=[prm_fcw])

        invf = A.alloc("invf", [1], F32, parts=64)
        P.dma("sync", invf.ap, cst_invf, writes=[invf])
        CH = 2048
        C1 = 6.28125
        C2 = 2.0 * PI - C1
        PIC = 3.1415925
        posi = A.alloc("posi", [CH], I32, parts=64)
        kint = A.alloc("kint", [CH], I32, parts=64)
        ang = A.alloc("ang", [CH], F32, parts=64)
        targ = A.alloc("targ", [CH], F32, parts=64)
        rem = A.alloc("rem", [CH], F32, parts=64)
        tout = A.alloc("tout", [CH], F32, parts=64)
        for c0 in range(0, S, CH):
            P.dma("sync", posi.ap, pos[:, c0:c0 + CH], writes=[posi])
            P.op("vector", lambda e: e.tensor_copy(out=ang.ap, in_=posi.ap), reads=[posi], writes=[ang])
            P.op("vector", lambda e: e.tensor_scalar(out=ang.ap, in0=ang.ap, scalar1=invf.ap[:, 0:1], scalar2=None, op0=ALU.mult),
                 reads=[ang, invf], writes=[ang])
            P.op("vector", lambda e: e.tensor_scalar(out=targ.ap, in0=ang.ap, scalar1=1.0 / (2.0 * PI), scalar2=None, op0=ALU.mult),
                 reads=[ang], writes=[targ])
            P.op("vector", lambda e: e.tensor_copy(out=kint.ap, in_=targ.ap), reads=[targ], writes=[kint])
            P.op("vector", lambda e: e.tensor_copy(out=targ.ap, in_=kint.ap), reads=[kint], writes=[targ])
            P.op("vector", lambda e: e.scalar_tensor_tensor(out=rem.ap, in0=targ.ap, scalar=-C1, in1=ang.ap, op0=ALU.mult, op1=ALU.add),
                 reads=[targ, ang], writes=[rem])
            P.op("vector", lambda e: e.scalar_tensor_tensor(out=rem.ap, in0=targ.ap, scalar=-C2, in1=rem.ap, op0=ALU.mult, op1=ALU.add),
                 reads=[targ, rem], writes=[rem])
            P.op("vector", lambda e: e.tensor_scalar(out=targ.ap, in0=rem.ap, scalar1=-PIC, scalar2=PIC, op0=ALU.max, op1=ALU.min),
                 reads=[rem], writes=[targ])
            P.op("scalar", lambda e: e.activation(out=tout.ap[0:32], in_=targ.ap[0:32], func=AF.Sin, scale=-1.0),
                 reads=[targ], writes=[tout])
            P.op("scalar", lambda e: e.activation(out=tout.ap[32:64], in_=targ.ap[32:64], func=AF.Sin, scale=1.0),
                 reads=[targ], writes=[tout], partial=True)
            P.dma("sync", cstab[1, :, c0:c0 + CH], tout.ap, reads=[tout], writes=[B_cstab], partial=True)
            P.op("vector", lambda e: e.tensor_scalar(out=rem.ap, in0=rem.ap, scalar1=0.5 * PI, scalar2=None, op0=ALU.add),
                 reads=[rem], writes=[rem])
            P.op("vector", lambda e: e.tensor_scalar(out=targ.ap, in0=rem.ap, scalar1=PI, scalar2=-2.0 * PI, op0=ALU.is_gt, op1=ALU.mult),
                 reads=[rem], writes=[targ])
            P.op("vector", lambda e: e.tensor_tensor(out=rem.ap, in0=rem.ap, in1=targ.ap, op=ALU.add), reads=[rem, targ], writes=[rem])
            P.op("vector", lambda e: e.tensor_scalar(out=targ.ap, in0=rem.ap, scalar1=-PIC, scalar2=PIC, op0=ALU.max, op1=ALU.min),
                 reads=[rem], writes=[targ])
            P.op("scalar", lambda e: e.activation(out=tout.ap, in_=targ.ap, func=AF.Sin, scale=1.0), reads=[targ], writes=[tout])
            P.dma("sync", cstab[0, :, c0:c0 + CH], tout.ap, reads=[tout], writes=[B_cstab], partial=True)
        if debug:
            P.dma("sync", dbg["cstab"], cstab, reads=[B_cstab], writes=[B_dbg["cstab"]], semname="d_dbg")
        A.release(m0)

        cast_state = {"k": 0}

        def cast_jobs(l):
            jobs = []
            for i in range(0, NB1, 3):
                jobs.append((w1q[l * NB1 + i:l * NB1 + i + 3], w1[l * NB1 + i:l * NB1 + i + 3], B_w1q[l]))
            for i in range(0, 16, 4):
                jobs.append((w5q[l * 16 + i:l * 16 + i + 4], w5[l * 16 + i:l * 16 + i + 4], B_w5q[l]))
            return jobs

        def cast_issue(job):
            k = cast_state["k"]
            cast_state["k"] = k + 1
            sname = f"d_cast{k % 6}"
            sm = P.sem(sname)
            if sm.v > 0:
                P.wait_toks("gpsimd", {sm: sm.v})
            P.dma("gpsimd", job[0], job[1], writes=[job[2]], semname=sname, partial=True)

        for job in cast_jobs(0):
            cast_issue(job)

        def tail_tiles(A, which, get_tile, nt=8):
            xb = [A.alloc(f"xb{i}", [D], BF16) for i in range(2)]
            slab = A.alloc("slab", [32, TB], BF16)
            for tt in range(nt):
                src = get_tile(tt)
                b = xb[tt % 2]
                P.op("scalar", lambda e: e.activation(out=b.ap, in_=src.ap, func=AF.Copy), reads=[src], writes=[b])
                for g4 in range(4):
                    pb = ps[(tt * 4 + g4) % 4]
                    pv = pb.ap.bitcast(BF16)
                    P.deps("tensor", reads=[b, ident], writes=[pb])
                    inst = None
                    for k in range(8):
                        blk = g4 * 8 + k
                        inst = P.emit("tensor", lambda e: e.transpose(out=pv[:, k * 128:(k + 1) * 128],
                                                                      in_=b.ap[:, blk * 128:(blk + 1) * 128],
                                                                      identity=ident.ap))
                    tok = P.tick("tensor", inst)
                    P.record(tok, reads=[b, ident], writes=[pb], eng="tensor")
                    dst = slab.ap[:, g4 * 8:(g4 + 1) * 8, (tt % 4) * 128:(tt % 4 + 1) * 128]
                    srcv = pv.rearrange("p (k t) -> p k t", k=8)
                    eng = "vector" if g4 % 2 == 0 else "scalar"
                    if eng == "vector":
                        P.op(eng, lambda e: e.tensor_copy(out=dst, in_=srcv), reads=[pb], writes=[slab], partial=True)
                    else:
                        P.op(eng, lambda e: e.activation(out=dst, in_=srcv, func=AF.Copy), reads=[pb], writes=[slab], partial=True)
                if tt % 4 == 3:
                    P.dma("sync", ag_in[which][tt // 4], slab.ap.rearrange("p a b -> p (a b)"),
                          reads=[slab], writes=[B_agin[which]], partial=True)
            P.cc("AllGather", ALU.bypass, ag_in2[which].opt(), xg2[which].opt(), reads=[B_agin[which]], writes=[B_xg[which]])

        m0 = A.mark()
        xin = [A.alloc(f"xin{i}", [D], F32) for i in range(2)]

        def get_x0(tt):
            b = xin[tt % 2]
            P.dma("sync", b.ap, xs[tt * 128:(tt + 1) * 128, :], writes=[b])
            return b

        tail_tiles(A, 0, get_x0)
        A.release(m0)
        P.barrier()

        for l in range(nlayers):
            next_jobs = cast_jobs(l + 1) if l + 1 < nlayers else []
            res_src = xs if l == 0 else xres
            B_res = None if l == 0 else B_xres

            m1 = A.mark()
            krT = A.alloc("krT", [S], BF16, parts=64)
            xT = [A.alloc(f"xT{i}", [32, TB], BF16) for i in range(2)]
            wt = [A.alloc(f"wt{i}", [32, 128], BF16) for i in range(3)]
            lat = A.alloc("lat", [12, TB], F32)
            sq = [A.alloc(f"sq{i}", [TB], BF16) for i in range(2)]
            rstd = A.alloc("rstd", [TB], F32)
            lnb = A.alloc("lnb", [8, TB], BF16)
            cs_c = A.alloc("cs_c", [TB], F32, parts=64)
            cs_s = A.alloc("cs_s", [TB], F32, parts=64)
            rt1 = A.alloc("rt1", [TB], F32, parts=64)
            rt2 = A.alloc("rt2", [TB], F32, parts=64)
            gct = A.alloc("gct", [TB], F32)
            u = A.alloc("u", [TB + 2], F32)
            cv = A.alloc("cv", [TB], F32)
            gcs = A.alloc("gcs", [TB], F32)
            carry = A.alloc("carry", [4, 2], F32)
            mt = [A.alloc(f"mt{i}", [4, TB], BF16) for i in range(2)]
            gt = [A.alloc(f"gt{i}", [4, TB], BF16) for i in range(2)]
            P.op("vector", lambda e: e.memset(carry.ap, 0.0), writes=[carry])
            wcount = 0
            pcount = 0
            PSQ, PSKV = ps[6], ps[7]
            for tb in range(NTB):
                if tb < len(next_jobs):
                    cast_issue(next_jobs[tb])
                xb_ = xT[tb % 2]
                P.dma("sync", xb_.ap.rearrange("p a b -> p (a b)"), xg[0][tb], reads=[B_xg[0]], writes=[xb_])
                P.dma("sync", cs_c.ap, cstab[0, :, tb * TB:(tb + 1) * TB], reads=[B_cstab], writes=[cs_c])
                P.dma("sync", cs_s.ap, cstab[1, :, tb * TB:(tb + 1) * TB], reads=[B_cstab], writes=[cs_s])
                m_ = mt[tb % 2]
                g_ = gt[tb % 2]
                for blk in range(NB1):
                    w_ = wt[wcount % 3]
                    wcount += 1
                    P.dma("sync", w_.ap.rearrange("p a b -> p (a b)"), w1q[l * NB1 + blk], reads=[B_w1q[l]], writes=[w_])
                    if blk == 12:
                        pa = ps[pcount % 4]; pcount += 1
                        pbk = ps[pcount % 4]; pcount += 1
                        for (pp, c0) in ((pa, 0), (pbk, 64)):
                            P.deps("tensor", reads=[w_, xb_], writes=[pp])
                            inst = None
                            for kc in range(32):
                                inst = P.emit("tensor", lambda e: e.matmul(pp.ap[0:64, :], lhsT=w_.ap[:, kc, c0:c0 + 64],
                                                                           rhs=xb_.ap[:, kc, :], start=(kc == 0), stop=(kc == 31)))
                            P.record(P.tick("tensor", inst), eng="tensor", reads=[w_, xb_], writes=[pp])
                        P.op("vector", lambda e: e.tensor_tensor(out=rt1.ap, in0=pa.ap[0:64, :], in1=cs_c.ap, op=ALU.mult),
                             reads=[pa, cs_c], writes=[rt1])
                        P.op("vector", lambda e: e.tensor_tensor(out=rt2.ap, in0=pbk.ap[0:64, :], in1=cs_s.ap, op=ALU.mult),
                             reads=[pbk, cs_s], writes=[rt2])
                        P.op("vector", lambda e: e.tensor_tensor(out=krT.ap[:, tb * TB:(tb + 1) * TB], in0=rt1.ap, in1=rt2.ap, op=ALU.add),
                             reads=[rt1, rt2], writes=[krT], partial=True)
                        continue
                    pb = ps[pcount % 4]; pcount += 1
                    P.deps("tensor", reads=[w_, xb_], writes=[pb])
                    inst = None
                    for kc in range(32):
                        inst = P.emit("tensor", lambda e: e.matmul(pb.ap, lhsT=w_.ap[:, kc, :], rhs=xb_.ap[:, kc, :],
                                                                   start=(kc == 0), stop=(kc == 31)))
                    P.record(P.tick("tensor", inst), eng="tensor", reads=[w_, xb_], writes=[pb])
                    if blk < 12:
                        first = blk in (0, 8)
                        last = blk in (7, 11)
                        pssq = PSQ if blk < 8 else PSKV
                        sq_ = sq[blk % 2]
                        P.op("scalar", lambda e: e.activation(out=lat.ap[:, blk, :], in_=pb.ap, func=AF.Copy),
                             reads=[pb], writes=[lat], partial=True)
                        P.op("scalar", lambda e: e.activation(out=sq_.ap, in_=pb.ap, func=AF.Square), reads=[pb], writes=[sq_])
                        P.deps("tensor", reads=[sq_, ones], writes=[pssq], partial=not first)
                        inst = P.emit("tensor", lambda e: e.matmul(pssq.ap, lhsT=ones.ap, rhs=sq_.ap, start=first, stop=last))
                        P.record(P.tick("tensor", inst), eng="tensor", reads=[sq_, ones], writes=[pssq])
                        if last:
                            nf = NQ if blk < 8 else NKV
                            P.op("scalar", lambda e: e.activation(out=rstd.ap, in_=pssq.ap, func=AF.Sqrt, bias=c_rms.ap[:, 0:1], scale=1.0 / nf),
                                 reads=[pssq, c_rms], writes=[rstd])
                            P.op("vector", lambda e: e.reciprocal(out=rstd.ap, in_=rstd.ap), reads=[rstd], writes=[rstd])
                            k0, nk = (0, 8) if blk < 8 else (8, 4)
                            for k in range(nk):
                                P.op("vector", lambda e: e.scalar_tensor_tensor(
                                    out=lnb.ap[:, k, :], in0=lat.ap[:, k0 + k, :],
                                    scalar=prm_nw.ap[:, l * 12 + k0 + k:l * 12 + k0 + k + 1], in1=rstd.ap,
                                    op0=ALU.mult, op1=ALU.mult), reads=[lat, rstd, prm_nw], writes=[lnb], partial=(k > 0))
                            if blk < 8:
                                P.dma("gpsimd", cqn[tb], lnb.ap.rearrange("p a b -> p (a b)"), reads=[lnb], writes=[B_cqn], partial=True)
                            else:
                                P.dma("gpsimd", ckvn[tb], lnb.ap[:, 0:4, :].rearrange("p a b -> p (a b)"), reads=[lnb], writes=[B_ckvn], partial=True)
                        continue
                    j, kind = divmod(blk - 13, 5)
                    if kind == 0:
                        P.op("scalar", lambda e: e.activation(out=gct.ap, in_=pb.ap, func=AF.Copy), reads=[pb], writes=[gct])
                    elif kind == 1:
                        P.op("vector", lambda e: e.tensor_copy(out=u.ap[:, 0:2], in_=carry.ap[:, j, :]), reads=[carry], writes=[u])
                        P.op("vector", lambda e: e.tensor_tensor(out=u.ap[:, 2:TB + 2], in0=gct.ap, in1=pb.ap, op=ALU.mult),
                             reads=[gct, pb], writes=[u], partial=True)
                        P.op("vector", lambda e: e.tensor_tensor(out=carry.ap[:, j, :], in0=gct.ap[:, TB - 2:TB], in1=pb.ap[:, TB - 2:TB], op=ALU.mult),
                             reads=[gct, pb], writes=[carry])
                        c3 = l * 12 + j * 3
                        P.op("vector", lambda e: e.tensor_scalar(out=cv.ap, in0=u.ap[:, 0:TB], scalar1=prm_cw.ap[:, c3:c3 + 1],
                                                                 scalar2=None, op0=ALU.mult), reads=[u, prm_cw], writes=[cv])
                        P.op("vector", lambda e: e.scalar_tensor_tensor(out=cv.ap, in0=u.ap[:, 1:TB + 1], scalar=prm_cw.ap[:, c3 + 1:c3 + 2],
                                                                        in1=cv.ap, op0=ALU.mult, op1=ALU.add), reads=[u, cv], writes=[cv])
                        P.op("vector", lambda e: e.scalar_tensor_tensor(out=cv.ap, in0=u.ap[:, 2:TB + 2], scalar=prm_cw.ap[:, c3 + 2:c3 + 3],
                                                                        in1=cv.ap, op0=ALU.mult, op1=ALU.add), reads=[u, cv], writes=[cv])
                    elif kind == 2:
                        P.op("vector", lambda e: e.tensor_tensor(out=cv.ap, in0=cv.ap, in1=pb.ap, op=ALU.mult), reads=[cv, pb], writes=[cv])
                    elif kind == 3:
                        bcol = l * 8 + 4 + j
                        P.op("scalar", lambda e: e.activation(out=gcs.ap, in_=pb.ap, func=AF.Sigmoid, bias=prm_bg.ap[:, bcol:bcol + 1], scale=1.0),
                             reads=[pb, prm_bg], writes=[gcs])
                        P.op("vector", lambda e: e.tensor_tensor(out=m_.ap[:, j, :], in0=cv.ap, in1=gcs.ap, op=ALU.mult),
                             reads=[cv, gcs], writes=[m_], partial=(j > 0))
                    else:
                        bcol = l * 8 + j
                        P.op("scalar", lambda e: e.activation(out=g_.ap[:, j, :], in_=pb.ap, func=AF.Sigmoid, bias=prm_bg.ap[:, bcol:bcol + 1], scale=1.0),
                             reads=[pb, prm_bg], writes=[g_], partial=(j > 0))
                P.dma("gpsimd", mconv[tb], m_.ap.rearrange("p a b -> p (a b)"), reads=[m_], writes=[B_mconv], partial=True)
                P.dma("gpsimd", gattn[tb], g_.ap.rearrange("p a b -> p (a b)"), reads=[g_], writes=[B_gattn], partial=True)
            A.release(m1)
            P.barrier()
            if debug and l == 0:
                P.dma("sync", dbg["cqn"], cqn, reads=[B_cqn], writes=[B_dbg["cqn"]], semname="d_dbg")
                P.dma("sync", dbg["ckvn"], ckvn, reads=[B_ckvn], writes=[B_dbg["ckvn"]], semname="d_dbg")
                P.dma("sync", dbg["mconv"], mconv, reads=[B_mconv], writes=[B_dbg["mconv"]], semname="d_dbg")
                P.dma("sync", dbg["gattn"], gattn, reads=[B_gattn], writes=[B_dbg["gattn"]], semname="d_dbg")
                P.dma("sync", dbg["kr"], krT.ap, reads=[krT], writes=[B_dbg["kr"]], semname="d_dbg")

            m2 = A.mark()
            krT = A.alloc("krT", [S], BF16, parts=64)
            minT = A.alloc("minT", [4, S], BF16)
            m2b = A.mark()
            wuq_h = [A.alloc(f"wuq_h{i}", [8 * 256], BF16) for i in range(2)]
            wukv_h = [A.alloc(f"wukv_h{i}", [4 * 256], BF16) for i in range(2)]
            KT = A.alloc("KT", [S], BF16)
            V = A.alloc("V", [S // 128, 128], BF16)
            ckv = [A.alloc(f"ckv{i}", [4, TB], BF16) for i in range(2)]
            cq = [A.alloc(f"cq{i}", [8, TB], BF16) for i in range(2)]
            qc = A.alloc("qc", [TB], F32, parts=64)
            qs = A.alloc("qs", [TB], F32, parts=64)
            qt1 = A.alloc("qt1", [TB], F32, parts=64)
            qt2 = A.alloc("qt2", [TB], F32, parts=64)
            qn = [A.alloc(f"qn{i}", [TB], BF16) for i in range(2)]
            qr = [A.alloc(f"qr{i}", [TB], BF16, parts=64) for i in range(2)]
            PT = [A.alloc(f"PT{i}", [TB], BF16) for i in range(4)]
            gab = [A.alloc(f"gab{i}", [TB], BF16) for i in range(2)]
            mcb = [A.alloc(f"mcb{i}", [TB], BF16) for i in range(2)]
            rec = A.alloc("rec", [TB], F32)
            on = A.alloc("on", [TB], F32)
            PS_S = [ps[0], ps[1], ps[7]]
            PS_O, PS_D = ps[2], ps[3]
            PS_Q, PS_A, PS_B = ps[4], ps[5], ps[6]
            PS_K = ps[7]
            ldc = 0
            for h in range(4):
                wuq_sb = wuq_h[h % 2]
                wukv_sb = wukv_h[h % 2]
                P.dma("gpsimd", wuq_sb.ap, wuq[l * 4 + h], writes=[wuq_sb])
                P.dma("gpsimd", wukv_sb.ap, wukv[l * 4 + h], writes=[wukv_sb])
                wq3 = wuq_sb.ap.rearrange("p (k c) -> p k c", k=8)
                wkv3 = wukv_sb.ap.rearrange("p (k c) -> p k c", k=4)
                for tb in range(NTB):
                    c_ = ckv[tb % 2]
                    P.dma("sync", c_.ap.rearrange("p a b -> p (a b)"), ckvn[tb], reads=[B_ckvn], writes=[c_])
                    P.deps("tensor", reads=[c_, wukv_sb], writes=[PS_K])
                    inst = None
                    for kc in range(4):
                        inst = P.emit("tensor", lambda e: e.matmul(PS_K.ap, lhsT=wkv3[:, kc, 0:128], rhs=c_.ap[:, kc, :],
                                                                   start=(kc == 0), stop=(kc == 3)))
                    P.record(P.tick("tensor", inst), eng="tensor", reads=[c_, wukv_sb], writes=[PS_K])
                    P.op("scalar", lambda e: e.activation(out=KT.ap[:, tb * TB:(tb + 1) * TB], in_=PS_K.ap, func=AF.Copy),
                         reads=[PS_K], writes=[KT], partial=True)
                    pv_ = PS_Q
                    P.deps("tensor", reads=[c_, wukv_sb], writes=[pv_])
                    inst = None
                    for sub in range(4):
                        for kc in range(4):
                            inst = P.emit("tensor", lambda e: e.matmul(pv_.ap[:, sub * 128:(sub + 1) * 128],
                                                                       lhsT=c_.ap[:, kc, sub * 128:(sub + 1) * 128],
                                                                       rhs=wkv3[:, kc, 128:256], start=(kc == 0), stop=(kc == 3)))
                    P.record(P.tick("tensor", inst), eng="tensor", reads=[c_, wukv_sb], writes=[pv_])
                    P.op("vector", lambda e: e.tensor_copy(out=V.ap[:, tb * 4:(tb + 1) * 4, :],
                                                           in_=pv_.ap.rearrange("p (a b) -> p a b", a=4)),
                         reads=[pv_], writes=[V], partial=True)
                for qb in range(NTB):
                    cq_ = cq[ldc % 2]
                    ga_ = gab[ldc % 2]
                    mc_ = mcb[ldc % 2]
                    qn_ = qn[ldc % 2]
                    qr_ = qr[ldc % 2]
                    ldc += 1
                    P.dma("sync", cq_.ap.rearrange("p a b -> p (a b)"), cqn[qb], reads=[B_cqn], writes=[cq_])
                    P.dma("sync", qc.ap, cstab[0, :, qb * TB:(qb + 1) * TB], reads=[B_cstab], writes=[qc])
                    P.dma("sync", qs.ap, cstab[1, :, qb * TB:(qb + 1) * TB], reads=[B_cstab], writes=[qs])
                    P.dma("sync", ga_.ap, gattn[qb, :, h * TB:(h + 1) * TB], reads=[B_gattn], writes=[ga_])
                    P.dma("sync", mc_.ap, mconv[qb, :, h * TB:(h + 1) * TB], reads=[B_mconv], writes=[mc_])
                    for (pp, c0, m) in ((PS_Q, 0, 128), (PS_A, 128, 64), (PS_B, 192, 64)):
                        P.deps("tensor", reads=[cq_, wuq_sb], writes=[pp])
                        inst = None
                        for kc in range(8):
                            inst = P.emit("tensor", lambda e: e.matmul(pp.ap[0:m, :], lhsT=wq3[:, kc, c0:c0 + m], rhs=cq_.ap[:, kc, :],
                                                                       start=(kc == 0), stop=(kc == 7)))
                        P.record(P.tick("tensor", inst), eng="tensor", reads=[cq_, wuq_sb], writes=[pp])
                    P.op("scalar", lambda e: e.activation(out=qn_.ap, in_=PS_Q.ap, func=AF.Copy), reads=[PS_Q], writes=[qn_])
                    P.op("vector", lambda e: e.tensor_tensor(out=qt1.ap, in0=PS_A.ap[0:64, :], in1=qc.ap, op=ALU.mult),
                         reads=[PS_A, qc], writes=[qt1])
                    P.op("vector", lambda e: e.tensor_tensor(out=qt2.ap, in0=PS_B.ap[0:64, :], in1=qs.ap, op=ALU.mult),
                         reads=[PS_B, qs], writes=[qt2])
                    P.op("vector", lambda e: e.tensor_tensor(out=qr_.ap, in0=qt1.ap, in1=qt2.ap, op=ALU.add),
                         reads=[qt1, qt2], writes=[qr_])
                    nkt = 4 * qb + 4

                    def do_S(kt):
                        sb = PS_S[kt % 3]
                        P.deps("tensor", reads=[KT, krT, qn_, qr_], writes=[sb])
                        P.emit("tensor", lambda e: e.matmul(sb.ap, lhsT=KT.ap[:, kt * 128:(kt + 1) * 128], rhs=qn_.ap, start=True, stop=False))
                        inst = P.emit("tensor", lambda e: e.matmul(sb.ap, lhsT=krT.ap[:, kt * 128:(kt + 1) * 128], rhs=qr_.ap, start=False, stop=True))
                        P.record(P.tick("tensor", inst), eng="tensor", reads=[KT, krT, qn_, qr_], writes=[sb])
                        pt = PT[kt % 4]
                        P.op("scalar", lambda e: e.activation(out=pt.ap, in_=sb.ap, func=AF.Exp, scale=SCALE), reads=[sb], writes=[pt])
                        if kt >= 4 * qb:
                            jm = kt - 4 * qb
                            P.op("vector", lambda e: e.tensor_tensor(out=pt.ap, in0=pt.ap, in1=mask.ap[:, jm, :], op=ALU.mult),
                                 reads=[pt, mask], writes=[pt])

                    def do_PV(kt):
                        pt = PT[kt % 4]
                        first = (kt == 0)
                        last = (kt == nkt - 1)
                        P.deps("tensor", reads=[pt, V, ones], writes=[PS_O, PS_D], partial=not first)
                        P.emit("tensor", lambda e: e.matmul(PS_O.ap, lhsT=V.ap[:, kt, :], rhs=pt.ap, start=first, stop=last))
                        inst = P.emit("tensor", lambda e: e.matmul(PS_D.ap, lhsT=ones.ap, rhs=pt.ap, start=first, stop=last))
                        P.record(P.tick("tensor", inst), eng="tensor", reads=[pt, V, ones], writes=[PS_O, PS_D])

                    do_S(0)
                    do_S(1)
                    for kt in range(nkt):
                        if kt + 2 < nkt:
                            do_S(kt + 2)
                        do_PV(kt)
                    P.op("vector", lambda e: e.reciprocal(out=rec.ap, in_=PS_D.ap), reads=[PS_D], writes=[rec])
                    P.op("vector", lambda e: e.tensor_tensor(out=on.ap, in0=PS_O.ap, in1=rec.ap, op=ALU.mult), reads=[PS_O, rec], writes=[on])
                    P.op("vector", lambda e: e.tensor_tensor(out=on.ap, in0=on.ap, in1=ga_.ap, op=ALU.mult), reads=[on, ga_], writes=[on])
                    P.op("vector", lambda e: e.tensor_tensor(out=minT.ap[:, h, qb * TB:(qb + 1) * TB], in0=on.ap, in1=mc_.ap, op=ALU.add),
                         reads=[on, mc_], writes=[minT], partial=True)
            A.release(m2b)
            P.barrier()
            if debug and l == 0:
                P.dma("sync", dbg["minT"], minT.ap.rearrange("p a b -> p (a b)"), reads=[minT], writes=[B_dbg["minT"]], semname="d_dbg")

            def row_partial(A, ntt, lhs_fn, nk, w_sb, base_tt, rows):
                for tt in range(ntt):
                    for og in range(8):
                        row = rows[og // 4]
                        pb = ps[og % 4]
                        rd = lhs_fn(tt)
                        P.deps("tensor", reads=rd[0], writes=[pb])
                        inst = None
                        for k in range(nk):
                            inst = P.emit("tensor", lambda e: e.matmul(pb.ap, lhsT=rd[1](k), rhs=w_sb.ap[:, k, og * 512:(og + 1) * 512],
                                                                       start=(k == 0), stop=(k == nk - 1)))
                        P.record(P.tick("tensor", inst), eng="tensor", reads=rd[0], writes=[pb])
                        dst = row.ap[:, (og % 4) * 512:(og % 4 + 1) * 512]
                        if og % 2 == 0:
                            P.op("vector", lambda e: e.tensor_copy(out=dst, in_=pb.ap), reads=[pb], writes=[row], partial=(og % 4 > 0))
                        else:
                            P.op("scalar", lambda e: e.activation(out=dst, in_=pb.ap, func=AF.Copy), reads=[pb], writes=[row], partial=True)
                        if og % 4 == 3:
                            t0 = (base_tt + tt) * 128
                            c0 = (og // 4) * 2048
                            P.dma("gpsimd", rs_in[t0:t0 + 128, c0:c0 + 2048], row.ap, reads=[row], writes=[B_rsin], partial=True)

            m3 = A.mark()
            wo_sb = A.alloc("wo_sb", [4, D], BF16)
            P.dma("gpsimd", wo_sb.ap.rearrange("p a b -> p (a b)"), wo[l], writes=[wo_sb])
            rows = [A.alloc(f"row{i}", [D // 2], F32) for i in range(2)]
            row_partial(A, S // 128,
                        lambda tt: ([minT, wo_sb], lambda k: minT.ap[:, k, tt * 128:(tt + 1) * 128]),
                        4, wo_sb, 0, rows)
            P.cc("ReduceScatter", ALU.add, rs_in.opt(), rs_out.opt(), reads=[B_rsin], writes=[B_rsout])
            A.release(m2)
            P.barrier()
            if debug and l == 0:
                P.dma("sync", dbg["rs1"], rs_out, reads=[B_rsout], writes=[B_dbg["rs1"]], semname="d_dbg")

            def layernorm(A, k_g, k_b, res_ap, B_resbuf, which, final):
                m4 = A.mark()
                gt_ = A.alloc("ln_g", [D], F32)
                bt_ = A.alloc("ln_b", [D], F32)
                P.dma("sync", gt_.ap, lnp[l * 4 + k_g], writes=[gt_])
                P.dma("sync", bt_.ap, lnp[l * 4 + k_b], writes=[bt_])
                mix = [A.alloc(f"mix{i}", [D], F32) for i in range(2)]
                xr = [A.alloc(f"xr{i}", [D], F32) for i in range(2)]
                bst = A.alloc("bst", [48], F32)
                st = [A.alloc(f"st{i}", [8], F32) for i in range(2)]
                outs = []

                def issue_loads(tt):
                    mx = mix[tt % 2]
                    xr_ = xr[tt % 2]
                    P.dma("sync", mx.ap, rs_out[tt * 128:(tt + 1) * 128, :], reads=[B_rsout], writes=[mx])
                    P.dma("sync", xr_.ap, res_ap[tt * 128:(tt + 1) * 128, :], reads=([B_resbuf] if B_resbuf else []), writes=[xr_])

                def get_tile(tt):
                    mx = mix[tt % 2]
                    xr_ = xr[tt % 2]
                    s_ = st[tt % 2]
                    if tt == 0:
                        issue_loads(0)
                    if tt + 1 < 8:
                        issue_loads(tt + 1)
                    P.op("vector", lambda e: e.scalar_tensor_tensor(out=mx.ap, in0=xr_.ap, scalar=ALPHA, in1=mx.ap, op0=ALU.mult, op1=ALU.add),
                         reads=[xr_, mx], writes=[mx])
                    for k in range(8):
                        P.op("vector", lambda e: e.bn_stats(out=bst.ap[:, k * 6:(k + 1) * 6], in_=mx.ap[:, k * 512:(k + 1) * 512]),
                             reads=[mx], writes=[bst], partial=(k > 0))
                    P.op("vector", lambda e: e.bn_aggr(out=s_.ap[:, 0:2], in_=bst.ap), reads=[bst], writes=[s_])
                    P.op("scalar", lambda e: e.activation(out=s_.ap[:, 2:3], in_=s_.ap[:, 1:2], func=AF.Sqrt, bias=c_ln.ap[:, 0:1], scale=1.0),
                         reads=[s_, c_ln], writes=[s_])
                    P.op("vector", lambda e: e.reciprocal(out=s_.ap[:, 3:4], in_=s_.ap[:, 2:3]), reads=[s_], writes=[s_])
                    P.op("vector", lambda e: e.tensor_scalar(out=mx.ap, in0=mx.ap, scalar1=s_.ap[:, 0:1], scalar2=s_.ap[:, 3:4],
                                                             op0=ALU.subtract, op1=ALU.mult), reads=[mx, s_], writes=[mx])
                    P.op("gpsimd", lambda e: e.tensor_tensor(out=mx.ap, in0=mx.ap, in1=gt_.ap, op=ALU.mult), reads=[mx, gt_], writes=[mx])
                    P.op("gpsimd", lambda e: e.tensor_tensor(out=mx.ap, in0=mx.ap, in1=bt_.ap, op=ALU.add), reads=[mx, bt_], writes=[mx])
                    if final:
                        P.dma("sync", y[tt * 128:(tt + 1) * 128, :], mx.ap, reads=[mx], writes=[B_y], partial=True)
                    else:
                        P.dma("sync", xres[tt * 128:(tt + 1) * 128, :], mx.ap, reads=[mx], writes=[B_xres], partial=True)
                    return mx

                if final:
                    for tt in range(8):
                        get_tile(tt)
                else:
                    tail_tiles(A, which, get_tile)
                A.release(m4)

            layernorm(A, 0, 1, res_src, B_res, 1, False)
            P.barrier()
            if debug and l == 0:
                P.dma("sync", dbg["x1"], xres, reads=[B_xres], writes=[B_dbg["x1"]], semname="d_dbg")

            m5 = A.mark()
            wd_sb = A.alloc("wd_sb", [8, D], BF16)
            P.dma("gpsimd", wd_sb.ap.rearrange("p a b -> p (a b)"), wd[l], writes=[wd_sb])
            x1T = [A.alloc(f"x1T{i}", [32, TB], BF16) for i in range(2)]
            wt5 = [A.alloc(f"wt5_{i}", [32, 128], BF16) for i in range(2)]
            at = A.alloc("at", [TB + 2], F32)
            cv5 = A.alloc("cv5", [TB], F32)
            sl = A.alloc("sl", [TB], F32)
            fcar = A.alloc("fcar", [8, 2], F32)
            gT = [A.alloc("gT0", [8, TB], BF16)]
            rows5 = [A.alloc(f"row5_{i}", [D // 2], F32) for i in range(2)]
            P.op("vector", lambda e: e.memset(fcar.ap, 0.0), writes=[fcar])
            wcount = 0
            PA_, PU_ = ps[4], ps[5]
            for tb in range(NTB):
                xb_ = x1T[tb % 2]
                g_ = gT[0]
                P.dma("sync", xb_.ap.rearrange("p a b -> p (a b)"), xg[1][tb], reads=[B_xg[1]], writes=[xb_])
                for j in range(8):
                    for (half, pp) in ((0, PA_), (1, PU_)):
                        w_ = wt5[wcount % 2]
                        wcount += 1
                        P.dma("sync", w_.ap.rearrange("p a b -> p (a b)"), w5q[l * 16 + 2 * j + half], reads=[B_w5q[l]], writes=[w_])
                        P.deps("tensor", reads=[w_, xb_], writes=[pp])
                        inst = None
                        for kc in range(32):
                            inst = P.emit("tensor", lambda e: e.matmul(pp.ap, lhsT=w_.ap[:, kc, :], rhs=xb_.ap[:, kc, :],
                                                                       start=(kc == 0), stop=(kc == 31)))
                        P.record(P.tick("tensor", inst), eng="tensor", reads=[w_, xb_], writes=[pp])
                    c3 = l * 24 + j * 3
                    P.op("vector", lambda e: e.tensor_copy(out=at.ap[:, 0:2], in_=fcar.ap[:, j, :]), reads=[fcar], writes=[at])
                    P.op("vector", lambda e: e.tensor_copy(out=at.ap[:, 2:TB + 2], in_=PA_.ap), reads=[PA_], writes=[at], partial=True)
                    P.op("vector", lambda e: e.tensor_copy(out=fcar.ap[:, j, :], in_=PA_.ap[:, TB - 2:TB]), reads=[PA_], writes=[fcar])
                    P.op("vector", lambda e: e.tensor_scalar(out=cv5.ap, in0=at.ap[:, 0:TB], scalar1=prm_fcw.ap[:, c3:c3 + 1],
                                                             scalar2=None, op0=ALU.mult), reads=[at, prm_fcw], writes=[cv5])
                    P.op("vector", lambda e: e.scalar_tensor_tensor(out=cv5.ap, in0=at.ap[:, 1:TB + 1], scalar=prm_fcw.ap[:, c3 + 1:c3 + 2],
                                                                    in1=cv5.ap, op0=ALU.mult, op1=ALU.add), reads=[at, cv5], writes=[cv5])
                    P.op("vector", lambda e: e.scalar_tensor_tensor(out=cv5.ap, in0=at.ap[:, 2:TB + 2], scalar=prm_fcw.ap[:, c3 + 2:c3 + 3],
                                                                    in1=cv5.ap, op0=ALU.mult, op1=ALU.add), reads=[at, cv5], writes=[cv5])
                    P.op("scalar", lambda e: e.activation(out=sl.ap, in_=cv5.ap, func=AF.Silu), reads=[cv5], writes=[sl])
                    P.op("vector", lambda e: e.tensor_tensor(out=g_.ap[:, j, :], in0=sl.ap, in1=PU_.ap, op=ALU.mult),
                         reads=[sl, PU_], writes=[g_], partial=(j > 0))
                row_partial(A, 4,
                            lambda tt: ([g_, wd_sb], lambda k: g_.ap[:, k, tt * 128:(tt + 1) * 128]),
                            8, wd_sb, tb * 4, rows5)
            P.cc("ReduceScatter", ALU.add, rs_in.opt(), rs_out.opt(), reads=[B_rsin], writes=[B_rsout])
            A.release(m5)
            P.barrier()

            layernorm(A, 2, 3, xres, B_xres, 0, l == nlayers - 1)
            P.barrier()

        P.final_wait()
        return A.peak

    with nc.sbuf_tensor("arena", [128, ARENA_BYTES // 2], BF16) as arena:
        psum_cms = [nc.psum_tensor(f"psum{i}", [128, 512], F32) for i in range(8)]
        psums = [cm.__enter__() for cm in psum_cms]
        with nc.Block() as block:
            for en in Prog.ENG:
                def body(e, en=en):
                    P.begin_pass(en, e)
                    program(P, arena[:], [p[:] for p in psums])
                getattr(block, en)(body)
        for cm in reversed(psum_cms):
            cm.__exit__(None, None, None)
    return nc


def _tile_cols(w, cols):
    K = w.shape[0]
    sub = w[:, cols]
    return np.ascontiguousarray(sub.reshape(K // 128, 128, len(cols)).transpose(1, 0, 2))


def _tile_rows(w, r0, nr):
    sub = w[r0:r0 + nr]
    return np.ascontiguousarray(sub.reshape(nr // 128, 128, w.shape[1]).transpose(1, 0, 2))


def prep_inputs(x, positions, w_in, b_gate, q_norm_w, kv_norm_w, w_uq, w_ukv, conv_w, w_o,
                ln1_g, ln1_b, w_ffn_in, ffn_conv_w, w_ffn_down, ln2_g, ln2_b, layers=None):
    f32 = np.float32
    if layers is None:
        layers = list(range(L))
    NL = len(layers)
    x2 = np.asarray(x, f32).reshape(S, D)
    posb = np.ascontiguousarray(np.broadcast_to(np.asarray(positions, np.int32).reshape(1, S), (64, S)))
    ident = np.eye(128, dtype=f32)
    kk = np.arange(128)[:, None]
    qq = np.arange(TB)[None, :]
    mask = np.concatenate([((j * 128 + kk) // 64 <= qq // 64).astype(f32) for j in range(4)], axis=1)
    invf1 = (1.0 / (THETA ** (np.arange(0, 64, 2, dtype=f32) / f32(64)))).astype(f32)
    invf = np.concatenate([invf1, invf1]).reshape(64, 1)
    perm = np.concatenate([np.arange(32, 64), np.arange(0, 32)])
    lnp_all = np.stack([np.stack([np.broadcast_to(np.asarray(a, f32)[l][None, :], (128, D))
                                  for a in (ln1_g, ln1_b, ln2_g, ln2_b)]) for l in layers]).reshape(NL * 4, 128, D)
    lnp_all = np.ascontiguousarray(lnp_all)
    maps = []
    for c in range(NCORE):
        w1 = np.empty((NL * NB1, 128, 32 * 128), f32)
        wuq_t = np.empty((NL * 4, 128, 8 * 256), f32)
        wukv_t = np.empty((NL * 4, 128, 4 * 256), f32)
        wo_t = np.empty((NL, 128, 4 * D), f32)
        w5 = np.empty((NL * 16, 128, 32 * 128), f32)
        wd_t = np.empty((NL, 128, 8 * D), f32)
        bgp = np.empty((128, NL * 8), f32)
        nwp = np.empty((128, NL * 12), f32)
        cwp = np.empty((128, NL * 12), f32)
        fcwp = np.empty((128, NL * 24), f32)
        for l, lsrc in enumerate(layers):
            wi = np.asarray(w_in[lsrc])
            blocks = []
            for i in range(8):
                blocks.append(np.arange(OFF_Q + 128 * i, OFF_Q + 128 * (i + 1)))
            for i in range(4):
                blocks.append(np.arange(OFF_KV + 128 * i, OFF_KV + 128 * (i + 1)))
            blocks.append(np.concatenate([OFF_KR + np.arange(64), OFF_KR + perm]))
            for j in range(4):
                cb = 4 * c + j
                for off in (OFF_CC, OFF_CH, OFF_CB, OFF_GC, OFF_GA):
                    blocks.append(np.arange(off + 128 * cb, off + 128 * (cb + 1)))
            for bi, cols in enumerate(blocks):
                w1[l * NB1 + bi] = _tile_cols(wi, cols).reshape(128, -1)
            wq = np.asarray(w_uq[lsrc])
            wkv = np.asarray(w_ukv[lsrc])
            for hh in range(4):
                h = 4 * c + hh
                qb = h * 192
                cols = np.concatenate([qb + np.arange(128), qb + 128 + np.arange(64), qb + 128 + perm])
                wuq_t[l * 4 + hh] = _tile_cols(wq, cols).reshape(128, -1)
                kb = h * 256
                cols = np.concatenate([kb + np.arange(128), kb + 128 + np.arange(128)])
                wukv_t[l * 4 + hh] = _tile_cols(wkv, cols).reshape(128, -1)
            wo_t[l] = _tile_rows(np.asarray(w_o[lsrc]), 512 * c, 512).reshape(128, -1)
            wf = np.asarray(w_ffn_in[lsrc])
            for j in range(8):
                fb = 8 * c + j
                w5[l * 16 + 2 * j] = _tile_cols(wf, np.arange(128 * fb, 128 * (fb + 1))).reshape(128, -1)
                w5[l * 16 + 2 * j + 1] = _tile_cols(wf, np.arange(DFF + 128 * fb, DFF + 128 * (fb + 1))).reshape(128, -1)
            wd_t[l] = _tile_rows(np.asarray(w_ffn_down[lsrc]), 1024 * c, 1024).reshape(128, -1)
            bgl = np.asarray(b_gate[lsrc], f32)
            for j in range(4):
                cb = 4 * c + j
                bgp[:, l * 8 + j] = bgl[128 * cb:128 * (cb + 1)]
                bgp[:, l * 8 + 4 + j] = bgl[D + 128 * cb:D + 128 * (cb + 1)]
            nwp[:, l * 12:l * 12 + 8] = np.asarray(q_norm_w[lsrc], f32).reshape(8, 128).T
            nwp[:, l * 12 + 8:l * 12 + 12] = np.asarray(kv_norm_w[lsrc], f32).reshape(4, 128).T
            cwl = np.asarray(conv_w[lsrc], f32)
            for j in range(4):
                cb = 4 * c + j
                cwp[:, l * 12 + 3 * j:l * 12 + 3 * j + 3] = cwl[:, 128 * cb:128 * (cb + 1)].T
            fl = np.asarray(ffn_conv_w[lsrc], f32)
            for j in range(8):
                fb = 8 * c + j
                fcwp[:, l * 24 + 3 * j:l * 24 + 3 * j + 3] = fl[:, 128 * fb:128 * (fb + 1)].T
        maps.append({
            "xs": np.ascontiguousarray(x2[c * TPC:(c + 1) * TPC]),
            "pos": posb, "w1": w1, "wuq": wuq_t, "wukv": wukv_t, "wo": wo_t, "w5": w5, "wd": wd_t,
            "bg": bgp, "nw": nwp, "cw": cwp, "fcw": fcwp, "lnp": lnp_all,
            "ident": ident, "mask": mask, "invf": invf,
        })
    return maps


FUSED = True


def kernel(**inputs):
    if FUSED:
        maps = prep_inputs(**inputs)
        nc = build(L, debug=False)
        res = run_bass_kernel_spmd(nc, maps, core_ids=list(range(NCORE)), trace=True)
        out = np.concatenate([res.results[c]["y"] for c in range(NCORE)], axis=0)
        return out.reshape(1, S, D).astype(np.float32)
    nc = build(1, debug=False)
    cur = np.asarray(inputs["x"], np.float32)
    args = dict(inputs)
    for l in range(L):
        args["x"] = cur
        maps = prep_inputs(**args, layers=[l])
        res = run_bass_kernel_spmd(nc, maps, core_ids=list(range(NCORE)), trace=True)
        cur = np.concatenate([res.results[c]["y"] for c in range(NCORE)], axis=0).reshape(1, S, D)
    return cur.astype(np.float32)
```
